# Optimizing a Trainium2 kernel written in Bass

```python
import jax, jax.numpy as jnp
from jax import lax
import numpy as np

D_MODEL = 1024
BATCH = 2
SEQ = 8192
DEPTH = 2

CHUNK = 64
LEFT_CHUNKS = 8
BAND = (LEFT_CHUNKS + 1) * CHUNK
HEAD_DIM = 64
N_HEADS_A = 8
N_HEADS_C = 8
WIDTH_A = N_HEADS_A * HEAD_DIM
WIDTH_C = N_HEADS_C * HEAD_DIM
POOL_WINDOWS = (2, 4, 8, 16)
N_POOL_GROUPS = 4
POOL_GROUP_DIM = 128
WIDTH_B = N_POOL_GROUPS * POOL_GROUP_DIM
REL_CLIP = 128
N_REL = 2 * REL_CLIP + 1
N_BRANCHES = 3
D_FF = 2816
Q_BLOCK = 128
N_MOD = 9
EPS = 1e-6
NEG_INF = -1e30

SPLIT_SIZES = (WIDTH_A, WIDTH_A, WIDTH_A, WIDTH_B, WIDTH_C, WIDTH_C, WIDTH_C, N_HEADS_C, N_BRANCHES * D_MODEL)
D_IN = sum(SPLIT_SIZES)
SPLIT_POINTS = tuple(int(p) for p in np.cumsum(SPLIT_SIZES)[:-1])

kernel_name = "hybrid_chunk_causal_encoder"


def rms_norm(x, gain):
    xf = x.astype(jnp.float32)
    y = xf * lax.rsqrt(jnp.mean(xf * xf, axis=-1, keepdims=True) + EPS)
    return (y * gain.astype(jnp.float32)).astype(x.dtype)


def modulate(h, shift, scale):
    return h * (1 + scale[:, None, :]) + shift[:, None, :]


def swiglu(h, w_gate, w_up, w_down):
    return (jax.nn.silu(h @ w_gate) * (h @ w_up)) @ w_down


def chunked_relpos_attention(q, k, v, gq, gk, rel_bias):
    B, S, H, dh = q.shape
    nc = S // CHUNK
    qf = rms_norm(q, gq).astype(jnp.float32) * (dh ** -0.5)
    kf = rms_norm(k, gk).astype(jnp.float32)
    qc = qf.reshape(B, nc, CHUNK, H, dh)
    pad = ((0, 0), (LEFT_CHUNKS * CHUNK, 0), (0, 0), (0, 0))
    kp = jnp.pad(kf, pad).reshape(B, nc + LEFT_CHUNKS, CHUNK, H, dh)
    vp = jnp.pad(v, pad).reshape(B, nc + LEFT_CHUNKS, CHUNK, H, dh)
    band_idx = np.arange(nc)[:, None] + np.arange(LEFT_CHUNKS + 1)[None, :]
    kb = kp[:, band_idx].reshape(B, nc, BAND, H, dh)
    vb = vp[:, band_idx].reshape(B, nc, BAND, H, dh)
    logits = jnp.einsum('bnqhd,bnkhd->bhnqk', qc, kb)
    rel = np.arange(CHUNK)[:, None] + LEFT_CHUNKS * CHUNK - np.arange(BAND)[None, :]
    rel_idx = np.clip(rel, -REL_CLIP, REL_CLIP) + REL_CLIP
    bias = rel_bias[:, rel_idx].astype(jnp.float32)
    valid = np.repeat(band_idx >= LEFT_CHUNKS, CHUNK, axis=1)
    logits = jnp.where(valid[None, None, :, None, :], logits + bias[None, :, None], NEG_INF)
    p = jax.nn.softmax(logits, axis=-1).astype(v.dtype)
    out = jnp.einsum('bhnqk,bnkhd->bnqhd', p, vb)
    return out.reshape(B, S, H * dh)


def multiscale_pool(u, w_pool, pool_scale):
    B, S, _ = u.shape
    uf = u.astype(jnp.float32)
    cs = jnp.concatenate([jnp.zeros((B, 1, WIDTH_B), jnp.float32), jnp.cumsum(uf, axis=1)], axis=1)
    t = jnp.arange(S, dtype=jnp.float32)
    diffs = []
    for g, w in enumerate(POOL_WINDOWS):
        sl = slice(g * POOL_GROUP_DIM, (g + 1) * POOL_GROUP_DIM)
        csg = cs[:, :, sl]
        upper = csg[:, 1:]
        lower = jnp.pad(csg[:, :S + 1 - w], ((0, 0), (w - 1, 0), (0, 0)))
        count = jnp.minimum(t + 1, float(w))[None, :, None]
        diffs.append((upper - lower) / count - uf[:, :, sl])
    d = jnp.stack(diffs, axis=2).astype(u.dtype)
    y = jnp.einsum('bsgc,gce->bsge', d, w_pool).reshape(B, S, WIDTH_B)
    return y * pool_scale


def forgetting_attention(q, k, v, gq, gk, f_logit):
    B, S, H, dh = q.shape
    qf = rms_norm(q, gq).astype(jnp.float32) * (dh ** -0.5)
    kf = rms_norm(k, gk).astype(jnp.float32)
    cum_logf = jnp.cumsum(jax.nn.log_sigmoid(f_logit.astype(jnp.float32)), axis=1)
    cum_k = cum_logf.transpose(0, 2, 1)
    kpos = jnp.arange(S)

    def block(i):
        start = i * Q_BLOCK
        qb = lax.dynamic_slice_in_dim(qf, start, Q_BLOCK, axis=1)
        cq = lax.dynamic_slice_in_dim(cum_k, start, Q_BLOCK, axis=2)
        logits = jnp.einsum('bqhd,bkhd->bhqk', qb, kf) + (cq[..., None] - cum_k[:, :, None, :])
        qpos = start + jnp.arange(Q_BLOCK)
        mask = kpos[None, :] <= qpos[:, None]
        logits = jnp.where(mask[None, None], logits, NEG_INF)
        p = jax.nn.softmax(logits, axis=-1).astype(v.dtype)
        return jnp.einsum('bhqk,bkhd->bqhd', p, v)

    out = lax.map(block, jnp.arange(S // Q_BLOCK))
    return out.transpose(1, 0, 2, 3, 4).reshape(B, S, H * dh)


def hybrid_mixer(h, w_in, qk_gain, rel_bias, forget_bias, w_pool, pool_scale, w_branch, w_out):
    B, S, _ = h.shape
    proj = h @ w_in
    qa, ka, va, ub, qc, kc, vc, f_lin, g_lin = jnp.split(proj, SPLIT_POINTS, axis=-1)
    ha = lambda t: t.reshape(B, S, N_HEADS_A, HEAD_DIM)
    hc = lambda t: t.reshape(B, S, N_HEADS_C, HEAD_DIM)
    ya = chunked_relpos_attention(ha(qa), ha(ka), ha(va), qk_gain[0], qk_gain[1], rel_bias)
    yb = multiscale_pool(ub, w_pool, pool_scale)
    yc = forgetting_attention(hc(qc), hc(kc), hc(vc), qk_gain[2], qk_gain[3], f_lin + forget_bias)
    ys = jnp.stack([ya, yb, yc], axis=2)
    branch = jnp.einsum('bsnw,nwd->bsnd', ys, w_branch)
    gates = jax.nn.sigmoid(g_lin.reshape(B, S, N_BRANCHES, D_MODEL))
    merged = jnp.sum(gates * branch, axis=2)
    return merged @ w_out


def setup_inputs(seed: int = 0) -> dict:
    key = jax.random.key(seed)
    ks = jax.random.split(key, 18)
    nrm = lambda k, shape, s: jax.random.normal(k, shape, jnp.float32) * s
    return {
        "x": nrm(ks[0], (BATCH, SEQ, D_MODEL), 1.0),
        "c": nrm(ks[1], (BATCH, D_MODEL), 1.0),
        "w_ada": nrm(ks[2], (DEPTH, D_MODEL, N_MOD * D_MODEL), 0.5 * D_MODEL ** -0.5),
        "b_ada": nrm(ks[3], (DEPTH, N_MOD * D_MODEL), 0.02),
        "norm_gain": 1.0 + nrm(ks[4], (DEPTH, 3, D_MODEL), 0.05),
        "ffn_w_gate": nrm(ks[5], (DEPTH, 2, D_MODEL, D_FF), D_MODEL ** -0.5),
        "ffn_w_up": nrm(ks[6], (DEPTH, 2, D_MODEL, D_FF), D_MODEL ** -0.5),
        "ffn_w_down": nrm(ks[7], (DEPTH, 2, D_FF, D_MODEL), D_FF ** -0.5),
        "w_in": nrm(ks[8], (DEPTH, D_MODEL, D_IN), D_MODEL ** -0.5),
        "qk_gain": 1.0 + nrm(ks[9], (DEPTH, 4, HEAD_DIM), 0.05),
        "rel_bias": nrm(ks[10], (DEPTH, N_HEADS_A, N_REL), 0.5),
        "forget_bias": 1.0 + 3.0 * jax.random.uniform(ks[11], (DEPTH, N_HEADS_C), jnp.float32),
        "w_pool": nrm(ks[12], (DEPTH, N_POOL_GROUPS, POOL_GROUP_DIM, POOL_GROUP_DIM), POOL_GROUP_DIM ** -0.5),
        "pool_scale": 1.0 + nrm(ks[13], (DEPTH, WIDTH_B), 0.05),
        "w_branch": nrm(ks[14], (DEPTH, N_BRANCHES, WIDTH_A, D_MODEL), WIDTH_A ** -0.5),
        "w_out": nrm(ks[15], (DEPTH, D_MODEL, D_MODEL), D_MODEL ** -0.5),
    }


def reference(x, c, w_ada, b_ada, norm_gain, ffn_w_gate, ffn_w_up, ffn_w_down, w_in, qk_gain,
              rel_bias, forget_bias, w_pool, pool_scale, w_branch, w_out):
    B = x.shape[0]
    c_act = jax.nn.silu(c)
    for l in range(DEPTH):
        mod = (c_act @ w_ada[l] + b_ada[l]).reshape(B, N_MOD, D_MODEL)
        h = modulate(rms_norm(x, norm_gain[l, 0]), mod[:, 0], mod[:, 1])
        x = x + 0.5 * mod[:, 2][:, None, :] * swiglu(h, ffn_w_gate[l, 0], ffn_w_up[l, 0], ffn_w_down[l, 0])
        h = modulate(rms_norm(x, norm_gain[l, 1]), mod[:, 3], mod[:, 4])
        x = x + mod[:, 5][:, None, :] * hybrid_mixer(h, w_in[l], qk_gain[l], rel_bias[l], forget_bias[l],
                                                      w_pool[l], pool_scale[l], w_branch[l], w_out[l])
        h = modulate(rms_norm(x, norm_gain[l, 2]), mod[:, 6], mod[:, 7])
        x = x + 0.5 * mod[:, 8][:, None, :] * swiglu(h, ffn_w_gate[l, 1], ffn_w_up[l, 1], ffn_w_down[l, 1])
    return x
```

```python
import numpy as np
import ml_dtypes
import concourse.bass as bass
import concourse.mybir as mybir
from concourse.bass_utils import run_bass_kernel_spmd

F32 = mybir.dt.float32
BF16 = mybir.dt.bfloat16
ALU = mybir.AluOpType
AF = mybir.ActivationFunctionType

D = 1024
T = 2048
S = 8192
NL = 2
DFF = 2816
NFC = 22
DIN = 6664
EPS = 1e-6
R1 = 3600
R2 = 1536
SEC = dict(QA=0, KA=1, VA=2, UB=3, QC=4, KC=5, VC=6)


def s1row(sec, j):
    return (j // 2) * 1792 + SEC[sec] * 256 + (j % 2) * 128


def s2row(n, q):
    return (q // 2) * 768 + n * 256 + (q % 2) * 128
SLOT = 4608
NSLOT = 3
NSM = 128


def weight_items(nl=NL):
    def ada(l):
        return [(("ada", m), l, 8 * 512) for m in range(18)]

    def ffn(l, i):
        out = []
        for g in range(2):
            for k in range(6):
                nf = 2 if k < 5 else 1
                out.append((("gu", i, g, k), l, 2 * nf * 8 * 128))
            for q in range(4):
                out.append((("dn", i, g, q), l, 2 * 11 * 128))
        return out

    items = []
    for l in range(nl):
        if l == 0:
            items += ada(0)
        items += ffn(l, 0)
        items += [(("inf",), l, 64), (("in", "QC"), l, 4096), (("in", "KC"), l, 4096), (("inv", "VC"), l, 4096),
                  (("in", "UB"), l, 4096), (("in", "QA"), l, 4096), (("in", "KA"), l, 4096), (("inv", "VA"), l, 4096)]
        if l + 1 < nl:
            items += ada(l + 1)
        for hf in range(2):
            for dc in range(8):
                items.append((("mix", hf, dc), l, 3 * 12 * 128))
            for q in range(2):
                items.append((("wo", hf, q), l, 8 * 512))
        items += ffn(l, 1)
    return items


def host_item(name, l, W):
    k = name[0]
    if k == "ada":
        m = name[1]
        a = W["w_ada"][l].reshape(8, 128, 18, 512)[:, :, m, :]
        return a.transpose(1, 0, 2).reshape(128, -1)
    if k == "gu":
        _, i, g, kk = name
        nf = 2 if kk < 5 else 1
        f0 = g * 11 + kk * 2
        out = []
        for w in (W["ffn_w_gate"][l, i], W["ffn_w_up"][l, i]):
            a = w.reshape(8, 128, NFC, 128)[:, :, f0:f0 + nf, :]
            out.append(a.transpose(1, 2, 0, 3))
        return np.stack(out, axis=1).reshape(128, -1)
    if k == "dn":
        _, i, g, q = name
        a = W["ffn_w_down"][l, i].reshape(2, 11, 128, 4, 2, 128)[g, :, :, q]
        return a.transpose(1, 2, 0, 3).reshape(128, -1)
    if k == "in":
        sec = name[1]
        c0 = {"QA": 0, "KA": 512, "UB": 1536, "QC": 2048, "KC": 2560}[sec]
        a = W["w_in"][l][:, c0:c0 + 512].reshape(8, 128, 4, 128)
        return a.transpose(1, 2, 0, 3).reshape(128, -1)
    if k == "inv":
        c0 = {"VA": 1024, "VC": 3072}[name[1]]
        a = W["w_in"][l][:, c0:c0 + 512].reshape(8, 128, 512)
        return a.transpose(1, 0, 2).reshape(128, -1)
    if k == "inf":
        a = W["w_in"][l][:, 3584:3592].reshape(8, 128, 8)
        return a.transpose(1, 0, 2).reshape(128, -1)
    if k == "mix":
        _, hf, dc = name
        parts = []
        for n in range(3):
            c0 = 3592 + n * 1024 + dc * 128
            a = W["w_in"][l][:, c0:c0 + 128].reshape(8, 128, 128).transpose(1, 0, 2)
            b = W["w_branch"][l, n][:, dc * 128:(dc + 1) * 128].reshape(4, 128, 128).transpose(1, 0, 2)
            parts.append(np.concatenate([a, b], axis=1))
        return np.stack(parts, axis=1).reshape(128, -1)
    if k == "wo":
        _, hf, q = name
        a = W["w_out"][l][:, q * 512:(q + 1) * 512].reshape(8, 128, 512)
        return a.transpose(1, 0, 2).reshape(128, -1)
    raise KeyError(name)


class Ins:
    __slots__ = ("eng", "fn", "waits", "inc", "idx", "dma", "cnt", "tag")

    def __init__(self, eng, fn, dma=None):
        self.eng, self.fn, self.dma = eng, fn, dma
        self.waits = []
        self.inc = False
        self.idx = -1
        self.cnt = 0


ENGS = ("pe", "act", "dve", "pool", "sp")


class Prog:
    def __init__(self, nc):
        self.nc = nc
        self.q = {e: [] for e in ENGS}
        self.res = {}
        self.chan = {}
        self.inc_by = {}

    def _deps(self, ins, rd, wr, wrp):
        evs = []
        for k in rd:
            st = self.res.setdefault(k, [{}, {}])
            evs += list(st[0].values())
        for k in wr:
            st = self.res.setdefault(k, [{}, {}])
            evs += list(st[0].values()) + list(st[1].values())
        for k in wrp:
            st = self.res.setdefault(k, [{}, {}])
            evs += list(st[1].values())
        for ev in evs:
            if ev[0] == "e":
                o = ev[1]
                if o is ins:
                    continue
                if o.eng == ins.eng and ins.eng == "pe":
                    continue
                if o.eng == ins.eng and ins.dma is None and o.dma is None and ev[2] != "w":
                    continue
                o.inc = True
            ins.waits.append(ev)
        me_w = self._event(ins, "w")
        me_r = self._event(ins, "r")
        for k in rd:
            self._add(self.res[k][1], me_r)
        for k in wr:
            self.res[k] = [{}, {}]
            self._add(self.res[k][0], me_w)
        for k in wrp:
            self.res[k][1] = {}
            self._add(self.res[k][0], me_w)

    def _event(self, ins, kind):
        if ins.dma is not None:
            return ("d", ins.dma[0], ins.dma[1])
        return ("e", ins, kind)

    @staticmethod
    def _add(dct, ev):
        if ev[0] == "e":
            key = ("e", ev[1].eng)
            old = dct.get(key)
            if old is None or old[1].idx < ev[1].idx:
                dct[key] = ev
        else:
            key = ("d", ev[1])
            old = dct.get(key)
            if old is None or old[2] < ev[2]:
                dct[key] = ev

    def op(self, eng, fn, rd=(), wr=(), wrp=()):
        ins = Ins(eng, fn)
        ins.idx = len(self.q[eng])
        self.q[eng].append(ins)
        self._deps(ins, rd, wr, wrp)
        return ins

    def dma(self, q, out, in_, ch, rd=(), wr=(), wrp=(), inc=16, fn=None):
        c = self.chan.setdefault(ch, [None, 0])
        c[1] += inc
        if fn is None:
            fn = lambda e, out=out, in_=in_: e.dma_start(out=out, in_=in_)
        ins = Ins(q, fn, dma=(ch, c[1], inc))
        ins.tag = (str(getattr(out, "shape", None)), str(getattr(in_, "shape", None)))
        ins.idx = len(self.q[q])
        self.q[q].append(ins)
        self._deps(ins, rd, wr, wrp)
        return ins

    def barrier(self):
        last = {}
        for e in ENGS:
            for ins in reversed(self.q[e]):
                if ins.dma is None and ins.fn is not None:
                    last[e] = ins
                    ins.inc = True
                    break
        chans = {ch: c[1] for ch, c in self.chan.items()}
        for e in ENGS:
            ins = Ins(e, None)
            ins.idx = len(self.q[e])
            for o in last.values():
                if o.eng != e or e != "pe":
                    ins.waits.append(("e", o, "w"))
            for ch, v in chans.items():
                ins.waits.append(("d", ch, v))
            self.q[e].append(ins)

    def emit(self, stack):
        nc = self.nc
        esem = {e: stack.enter_context(nc.semaphore("es_" + e)) for e in ENGS if e != "sp"}
        for n_, (ch, c) in enumerate(self.chan.items()):
            c[0] = stack.enter_context(nc.semaphore("ch_%d" % n_))
        for e in ENGS:
            n = 0
            for ins in self.q[e]:
                if ins.dma is None and ins.inc and ins.fn is not None:
                    n += 1
                    ins.cnt = n
                elif ins.dma is None:
                    ins.cnt = -1
        print("SEM COUNTS", {e: max([i.cnt for i in self.q[e]] + [0]) for e in ENGS}, {str(ch): c[1] for ch, c in self.chan.items()})
        block = stack.enter_context(nc.Block())
        prog = self

        def run(engname):
            def body(eng):
                waited = {}
                for ins in prog.q[engname]:
                    need = {}
                    for ev in ins.waits:
                        if ev[0] == "e":
                            o = ev[1]
                            assert o.cnt > 0, "wait on non-incrementing instruction"
                            sem, val = esem[o.eng], o.cnt
                        else:
                            sem, val = prog.chan[ev[1]][0], ev[2]
                        k = id(sem)
                        if waited.get(k, 0) >= val:
                            continue
                        if k not in need or need[k][1] < val:
                            need[k] = (sem, val)
                    for k, (sem, val) in need.items():
                        eng.wait_ge(sem, val)
                        waited[k] = val
                    if ins.fn is None:
                        continue
                    try:
                        bi = ins.fn(eng)
                    except Exception:
                        print("EMIT FAIL", engname, ins.idx, ins.dma, getattr(ins, "tag", None))
                        raise
                    if ins.dma is not None:
                        bi.then_inc(prog.chan[ins.dma[0]][0], ins.dma[2])
                    elif ins.inc:
                        bi.then_inc(esem[engname], 1)
            return body

        block.tensor(run("pe"))
        block.scalar(run("act"))
        block.vector(run("dve"))
        block.gpsimd(run("pool"))
        block.sync(run("sp"))


def build_nc(nlayers=NL, stop=None, dbg=False):
    from contextlib import ExitStack
    nc = bass.Bass("TRN2", target_bir_lowering=False)
    items = weight_items(nlayers)
    offs = []
    o = 0
    for (_, _, cols) in items:
        offs.append(o)
        o += 128 * cols
    WTOT = o

    xT_d = nc.dram_tensor("xT", [D, T], F32, kind="ExternalInput").ap()
    cT_d = nc.dram_tensor("cT", [128, 8], F32, kind="ExternalInput").ap()
    wflat = nc.dram_tensor("wflat", [WTOT], F32, kind="ExternalInput").ap()
    small_d = nc.dram_tensor("small", [NL, 128, NSM], F32, kind="ExternalInput").ap()
    wpool_d = nc.dram_tensor("wpool", [NL, 128, 128], F32, kind="ExternalInput").ap()
    rb_d = nc.dram_tensor("rb", [128, NL * 2 * 5 * 128], F32, kind="ExternalInput").ap()
    masks_d = nc.dram_tensor("masks", [128, 4 * 512 + 5 * 128], BF16, kind="ExternalInput").ap()
    tri_d = nc.dram_tensor("tri", [32, 32], F32, kind="ExternalInput").ap()
    out_d = nc.dram_tensor("outT", [D, T], F32, kind="ExternalOutput").ap()
    if dbg:
        dbg1 = nc.dram_tensor("dbg1", [3584, 2048], BF16, kind="ExternalOutput").ap()
        dbg2 = nc.dram_tensor("dbg2", [R2, 2048], BF16, kind="ExternalOutput").ap()
    send1 = nc.dram_tensor("send1", [3584, 2048], BF16)
    recv1 = nc.dram_tensor("recv1", [4 * 3584, 2048], BF16)
    send1l = nc.dram_tensor("send1l", [16, 2048], BF16)
    recv1l = nc.dram_tensor("recv1l", [4 * 16, 2048], BF16)
    send2 = nc.dram_tensor("send2", [R2, 2048], BF16)
    recv2 = nc.dram_tensor("recv2", [4 * R2, 2048], BF16)
    cqk = nc.dram_tensor("cqk", [2 * 6, S], BF16)
    loc1 = nc.dram_tensor("loc1", [128, 28, 2048], BF16).ap().rearrange("p (s r) t -> p s r t", r=4)
    loclf = nc.dram_tensor("loclf", [2, 4, 2048], F32).ap()
    loc2 = nc.dram_tensor("loc2", [128, 12, 2048], BF16).ap()
    jv = nc.partition_id() % 4

    stack = ExitStack()
    with stack:
        def sb(name, shape, dt):
            return stack.enter_context(nc.sbuf_tensor(name, shape, dt))

        xT = sb("xT_sb", [128, 8, T], F32)
        ring = sb("ring", [128, NSLOT, SLOT], BF16)
        scr = sb("scr", [128, 38912], BF16)
        ps = stack.enter_context(nc.psum_tensor("ps", [128, 4096], F32))
        cT = sb("cT_sb", [128, 8], F32)
        cact = sb("cact", [128, 8], BF16)
        small = sb("small_sb", [128, NL, NSM], F32)
        modsb_a = sb("modsb", [128, NL, 72], F32)
        coefA_a = sb("coefA", [128, NL, 24], F32)
        ghalf_a = sb("ghalf", [128, NL, 24], F32)
        gq_a = sb("gq", [128, NL, 4], F32)
        nfb_a = sb("nfb", [128, NL, 1], F32)
        onesD = sb("onesD", [128, 128], BF16)
        blk64 = sb("blk64", [128, 128], BF16)
        ones32 = sb("ones32", [128, 128], F32)
        masks = sb("masks_sb", [128, 4 * 512 + 5 * 128], BF16)
        wpool = sb("wpool_sb", [128, NL, 128], BF16)
        Eh = sb("Eh", [128, NL * 2 * 5 * 128], BF16)
        sq = sb("sq", [128, 4, 512], BF16)
        rstd = sb("rstd", [128, 2, 512], F32)
        tmpf = sb("tmpf", [128, 2, 512], F32)
        sgt = sb("sgt", [128, 2, 512], BF16)
        f32a = sb("f32a", [128, 2, 512], F32)
        carry = sb("carry", [128, 1], F32)
        epsb = sb("epsb", [128, 1], F32)
        tri = sb("tri_sb", [32, 32], F32)
        segoff = sb("segoff", [128, 1], F32)

        P = Prog(nc)
        PS = lambda b: ps[:, b * 512:(b + 1) * 512]
        pk = lambda b: ("ps", b)

        hT = scr[:, 0:16384].rearrange("p (c t) -> p c t", c=8)
        aT = scr[:, 16384:16384 + 22528].rearrange("p (f t) -> p f t", f=11)
        stg = scr[:, 16384:16384 + 4096].rearrange("p (b c) -> p b c", b=2)
        rbf = scr[:, 0:2 * NL * 2 * 5 * 128].bitcast(F32)
        QT = scr[:, 0:8192]
        KT = scr[:, 8192:16384]
        VP = scr[:, 16384:16384 + 64 * 65].rearrange("p (k c) -> p k c", c=65)
        o0 = 16384 + 64 * 65 + 32
        Pt = scr[:, o0:o0 + 4 * 640].rearrange("p (b c) -> p b c", b=4)
        o0 += 4 * 640
        Pe = scr[:, o0:o0 + 2 * 640].rearrange("p (b c) -> p b c", b=2)
        o0 += 2 * 640
        yst = scr[:, o0:o0 + 2 * 512].rearrange("p (b c) -> p b c", b=2)
        o0 += 2 * 512
        o0 = (o0 + 63) // 64 * 64
        bwork = scr[:, o0:38912]
        hTh = scr[:, 0:8192].rearrange("p (c t) -> p c t", c=8)
        ysb = scr[:, 8192:8192 + 12288].rearrange("p (n t) -> p n t", n=12)
        mrg = scr[:, 20480:20480 + 8192].rearrange("p (c t) -> p c t", c=8)

        wstate = {"loaded": 0, "next": 0}

        def wload(i):
            name, l, cols = items[i]
            s = i % NSLOT
            src = wflat[offs[i]:offs[i] + 128 * cols].rearrange("(p c) -> p c", p=128)
            P.dma("pool", ring[:, s, 0:cols], src, ("w", s), wr=[("ring", s)])

        def wget(name, l):
            i = wstate["next"]
            assert items[i][0] == name and items[i][1] == l, (items[i], name, l)
            while wstate["loaded"] < min(len(items), i + NSLOT):
                if nlayers < NL and items[wstate["loaded"]][1] >= nlayers:
                    break
                wload(wstate["loaded"])
                wstate["loaded"] += 1
            wstate["next"] = i + 1
            s = i % NSLOT
            return ring[:, s, :], ("ring", s)

        for c in range(8):
            P.dma("sp", xT[:, c, :], xT_d[c * 128:(c + 1) * 128, :], "x%d" % c, wr=[("x", c)])
        P.dma("sp", cT[:], cT_d, "c0", wr=["cT"])
        P.dma("sp", small[:], small_d.rearrange("l p n -> p l n"), "c1", wr=["small"])
        P.dma("sp", masks[:], masks_d, "c2", wr=["masks"])
        P.dma("sp", tri[:], tri_d, "c5", wr=["tri"])
        P.dma("sp", rbf[:], rb_d, "c3", wr=["rbf"])
        P.dma("pool", wpool[:], wpool_d.rearrange("l p n -> p l n"), "c4", wr=["wpool"])
        P.op("act", lambda e: e.activation(out=cact[:], in_=cT[:], func=AF.Silu), rd=["cT"], wr=["cact"])
        P.op("dve", lambda e: e.memset(onesD[:], 1.0 / D), wr=["onesD"])
        P.op("dve", lambda e: e.memset(blk64[:], 0.0), wr=["blk64"])
        P.op("dve", lambda e: e.memset(blk64[0:64, 0:64], 1.0 / 64), rd=["blk64"], wrp=["blk64"])
        P.op("dve", lambda e: e.memset(blk64[64:128, 64:128], 1.0 / 64), rd=["blk64"], wrp=["blk64"])
        P.op("dve", lambda e: e.memset(ones32[:], 1.0), wr=["ones32"])
        P.op("dve", lambda e: e.memset(epsb[:], EPS), wr=["epsb"])
        P.op("act", lambda e: e.activation(out=Eh[:], in_=rbf[:], func=AF.Exp), rd=["rbf"], wr=["Eh"])
        for l in range(NL):
            for hh in range(2):
                sl = slice((l * 2 + hh) * 640, (l * 2 + hh + 1) * 640)
                P.op("dve", lambda e, sl=sl: e.tensor_tensor(out=Eh[:, sl], in0=Eh[:, sl], in1=masks[:, 2048:2688],
                                                             op=ALU.mult), rd=["Eh", "masks"], wrp=["Eh"])

        SM = lambda l, a, b: small[:, l, a:b]
        bank_rr = {"i": 0}

        ccn = {"i": 0}

        def allgather(sd, rv, ck, rdk, wrk):
            ccn["i"] += 1
            P.dma("pool", None, None, "cc%d" % (ccn["i"] % 4), rd=rdk, wrp=wrk, inc=1,
                  fn=lambda e: e.collective_compute(
                      "AllGather", ALU.bypass, replica_groups=[[0, 1, 2, 3], [4, 5, 6, 7]],
                      ins=[sd[ck * 256:(ck + 1) * 256, :]], outs=[rv[ck * 1024:(ck + 1) * 1024, :]]))

        def mod_layer(l):
            modsb, coefA, ghalf, gq, nfb = modsb_a[:, l, :], coefA_a[:, l, :], ghalf_a[:, l, :], gq_a[:, l, :], nfb_a[:, l, :]
            for m in range(18):
                W, wk = wget(("ada", m), l)
                Wv = W[:, 0:4096].rearrange("p (k c) -> p k c", k=8)
                for oc in range(4):
                    col = m * 4 + oc
                    for kc in range(8):
                        P.op("pe", lambda e, Wv=Wv, oc=oc, kc=kc, col=col: e.matmul(
                            ps[:, col:col + 1], lhsT=Wv[:, kc, oc * 128:(oc + 1) * 128], rhs=cact[:, kc:kc + 1],
                            start=(kc == 0), stop=(kc == 7)),
                            rd=[wk, "cact"], wr=[pk(0)] if (m == 0 and oc == 0 and kc == 0) else (),
                            wrp=() if (m == 0 and oc == 0 and kc == 0) else [pk(0)])
            P.op("dve", lambda e: e.tensor_tensor(out=modsb[:], in0=ps[:, 0:72], in1=SM(l, 0, 72), op=ALU.add),
                 rd=[pk(0), "small"], wr=["mod"])
            for n in range(3):
                P.op("dve", lambda e, n=n: e.scalar_tensor_tensor(
                    out=coefA[:, n * 8:(n + 1) * 8], in0=modsb[:, (3 * n + 1) * 8:(3 * n + 2) * 8], scalar=1.0,
                    in1=SM(l, 72 + n * 8, 80 + n * 8), op0=ALU.add, op1=ALU.mult), rd=["mod", "small"], wrp=["coef"])
                P.op("dve", lambda e, n=n: e.tensor_scalar(
                    out=ghalf[:, n * 8:(n + 1) * 8], in0=modsb[:, (3 * n + 2) * 8:(3 * n + 3) * 8],
                    scalar1=(1.0 if n == 1 else 0.5), scalar2=None, op0=ALU.mult), rd=["mod"], wrp=["coef"])
            P.op("dve", lambda e: e.tensor_scalar(out=gq[:, 0:4], in0=SM(l, 96, 100), scalar1=1.0, scalar2=None,
                                                  op0=ALU.mult), rd=["small"], wrp=["coef"])
            for n in (0, 2):
                P.op("dve", lambda e, n=n: e.tensor_scalar(out=gq[:, n:n + 1], in0=SM(l, 96 + n, 97 + n), scalar1=0.125,
                                                           scalar2=None, op0=ALU.mult), rd=["small", "coef"], wrp=["coef"])
            P.op("dve", lambda e: e.tensor_scalar(out=nfb[:], in0=SM(l, 100, 101), scalar1=-1.0, scalar2=None,
                                                  op0=ALU.mult), rd=["small"], wrp=["coef"])


        def layer(l):
            modsb, coefA, ghalf, gq, nfb = modsb_a[:, l, :], coefA_a[:, l, :], ghalf_a[:, l, :], gq_a[:, l, :], nfb_a[:, l, :]
            if l == 0:
                mod_layer(0)

            def norm_mod(n, dst, tiles):
                for k, tt in enumerate(tiles):
                    ts = slice(tt * 512, (tt + 1) * 512)
                    b = 7
                    for c4 in range(2):
                        P.op("act", lambda e, ts=ts, c4=c4: e.activation(out=sq[:], in_=xT[:, c4 * 4:(c4 + 1) * 4, ts], func=AF.Square),
                             rd=[("x", c) for c in range(c4 * 4, c4 * 4 + 4)], wr=["sq"])
                        for c in range(c4 * 4, c4 * 4 + 4):
                            P.op("pe", lambda e, c=c, b=b: e.matmul(PS(b), lhsT=onesD[:], rhs=sq[:, c % 4, :], start=(c == 0),
                                                                   stop=(c == 7)),
                                 rd=["onesD", "sq"], wr=[pk(b)] if c == 0 else (), wrp=[pk(b)] if c else ())
                    r = k % 2
                    P.op("act", lambda e, r=r, b=b: e.activation(out=f32a[:, r, :], in_=PS(b), func=AF.Sqrt, bias=epsb[:, 0:1], scale=1.0),
                         rd=[pk(b), "epsb"], wr=[("f32a", r)])
                    P.op("dve", lambda e, r=r: e.reciprocal(out=rstd[:, r, :], in_=f32a[:, r, :]),
                         rd=[("f32a", r)], wr=[("rstd", r)])
                    for c in range(8):
                        u = c % 2
                        P.op("dve", lambda e, c=c, u=u, r=r, ts=ts: e.scalar_tensor_tensor(
                            out=tmpf[:, u, :], in0=xT[:, c, ts], scalar=coefA[:, n * 8 + c:n * 8 + c + 1],
                            in1=rstd[:, r, :], op0=ALU.mult, op1=ALU.mult),
                            rd=[("x", c), "coef", ("rstd", r)], wr=[("tmpf", u)])
                        P.op("act", lambda e, c=c, u=u, k=k: e.activation(
                            out=dst[:, c, k * 512:(k + 1) * 512], in_=tmpf[:, u, :], func=AF.Identity,
                            bias=modsb[:, 3 * n * 8 + c:3 * n * 8 + c + 1], scale=1.0),
                            rd=[("tmpf", u), "mod"], wrp=[("h", k)])

            def ffn(i, n):
                for g in range(2):
                    for fl in range(11):
                        if fl % 2 == 0:
                            W, wk = wget(("gu", i, g, fl // 2), l)
                            nf = 2 if fl // 2 < 5 else 1
                            Wv = W[:, 0:2 * nf * 1024].rearrange("p (u f k c) -> p u f k c", u=2, f=nf, k=8)
                        fi = fl % 2
                        for pair in range(2):
                            bs = 4 * (bank_rr["i"] % 2)
                            bank_rr["i"] += 1
                            for c in range(8):
                                for u in range(2):
                                    for t2 in range(2):
                                        b = bs + u * 2 + t2
                                        tt = pair * 2 + t2
                                        P.op("pe", lambda e, Wv=Wv, u=u, fi=fi, c=c, b=b, tt=tt: e.matmul(
                                            PS(b), lhsT=Wv[:, u, fi, c, :], rhs=hT[:, c, tt * 512:(tt + 1) * 512],
                                            start=(c == 0), stop=(c == 7)),
                                            rd=[wk, ("h", tt)], wr=[pk(b)] if c == 0 else (), wrp=[pk(b)] if c else ())
                            for t2 in range(2):
                                tt = pair * 2 + t2
                                P.op("act", lambda e, b=bs + t2, t2=t2: e.activation(out=sgt[:, t2, :], in_=PS(b), func=AF.Silu),
                                     rd=[pk(bs + t2)], wr=[("sgt", t2)])
                                P.op("dve", lambda e, b=bs + 2 + t2, t2=t2, tt=tt, fl=fl: e.tensor_tensor(
                                    out=aT[:, fl, tt * 512:(tt + 1) * 512], in0=PS(b), in1=sgt[:, t2, :], op=ALU.mult),
                                    rd=[pk(bs + 2 + t2), ("sgt", t2)], wr=[("a", fl, tt)])
                    for dc in range(8):
                        if dc % 2 == 0:
                            W, wk = wget(("dn", i, g, dc // 2), l)
                            Wv = W[:, 0:2816].rearrange("p (d f c) -> p d f c", d=2, f=11)
                        for pair in range(2):
                            bs = 4 * (bank_rr["i"] % 2)
                            bank_rr["i"] += 1
                            for fl in range(11):
                                for t2 in range(2):
                                    b = bs + t2
                                    tt = pair * 2 + t2
                                    P.op("pe", lambda e, Wv=Wv, dc=dc, fl=fl, b=b, tt=tt: e.matmul(
                                        PS(b), lhsT=Wv[:, dc % 2, fl, :], rhs=aT[:, fl, tt * 512:(tt + 1) * 512],
                                        start=(fl == 0), stop=(fl == 10)),
                                        rd=[wk, ("a", fl, tt)], wr=[pk(b)] if fl == 0 else (), wrp=[pk(b)] if fl else ())
                            for t2 in range(2):
                                tt = pair * 2 + t2
                                ts = slice(tt * 512, (tt + 1) * 512)
                                P.op("dve", lambda e, b=bs + t2, dc=dc, ts=ts: e.scalar_tensor_tensor(
                                    out=xT[:, dc, ts], in0=PS(b), scalar=ghalf[:, n * 8 + dc:n * 8 + dc + 1], in1=xT[:, dc, ts],
                                    op0=ALU.mult, op1=ALU.add),
                                    rd=[pk(bs + t2), "coef"], wr=[("x", dc)])

            norm_mod(0, hT, range(4))
            ffn(0, 0)
            if stop == "ffn1":
                return
            norm_mod(1, hT, range(4))
            sidx = {"i": 0}

            def feat_sec(sec):
                W, wk = wget(("in", sec), l)
                Wv = W[:, 0:4096].rearrange("p (o k c) -> p o k c", o=4, k=8)
                gi = {"QA": 0, "KA": 1, "QC": 2, "KC": 3}.get(sec)
                for oc in range(4):
                    si = sidx["i"] % 2
                    sidx["i"] += 1
                    for pair in range(2):
                        bs = 2 * (bank_rr["i"] % 3)
                        bank_rr["i"] += 1
                        for c in range(8):
                            for t2 in range(2):
                                tt = pair * 2 + t2
                                b = bs + t2
                                P.op("pe", lambda e, Wv=Wv, oc=oc, c=c, b=b, tt=tt: e.matmul(
                                    PS(b), lhsT=Wv[:, oc, c, :], rhs=hT[:, c, tt * 512:(tt + 1) * 512], start=(c == 0),
                                    stop=(c == 7)),
                                    rd=[wk, ("h", tt)], wr=[pk(b)] if c == 0 else (), wrp=[pk(b)] if c else ())
                        for t2 in range(2):
                            tt = pair * 2 + t2
                            b = bs + t2
                            dst = stg[:, si, tt * 512:(tt + 1) * 512]
                            if gi is None:
                                P.op("act", lambda e, b=b, dst=dst: e.activation(out=dst, in_=PS(b), func=AF.Identity),
                                     rd=[pk(b)], wrp=[("stg", si)])
                            else:
                                b2 = 6 + t2
                                P.op("act", lambda e, b=b, t2=t2: e.activation(out=sgt[:, t2, :], in_=PS(b), func=AF.Square),
                                     rd=[pk(b)], wr=[("sgt", t2)])
                                P.op("pe", lambda e, b2=b2, t2=t2: e.matmul(PS(b2), lhsT=blk64[:], rhs=sgt[:, t2, :], start=True,
                                                                            stop=True),
                                     rd=["blk64", ("sgt", t2)], wr=[pk(b2)])
                                P.op("act", lambda e, b2=b2, t2=t2: e.activation(out=f32a[:, t2, :], in_=PS(b2), func=AF.Sqrt,
                                                                                 bias=epsb[:, 0:1], scale=1.0),
                                     rd=[pk(b2), "epsb"], wr=[("f32a", t2)])
                                P.op("dve", lambda e, t2=t2: e.reciprocal(out=rstd[:, t2, :], in_=f32a[:, t2, :]),
                                     rd=[("f32a", t2)], wr=[("rstd", t2)])
                                P.op("dve", lambda e, b=b, t2=t2, dst=dst, gi=gi: e.scalar_tensor_tensor(
                                    out=dst, in0=PS(b), scalar=gq[:, gi:gi + 1], in1=rstd[:, t2, :], op0=ALU.mult, op1=ALU.mult),
                                    rd=[pk(b), "coef", ("rstd", t2)], wrp=[("stg", si)])
                    r0 = s1row(sec, oc)
                    P.dma("sp", send1[r0:r0 + 128, :], stg[:, si, :], ("stg", si), rd=[("stg", si)], wrp=[("s1", oc // 2, sec)])
                    if oc % 2 == 1:
                        ag_defer.append(lambda oc=oc, sec=sec: allgather(send1, recv1, (oc // 2) * 7 + SEC[sec],
                                                                          [("s1", oc // 2, sec)], [("recv1", SEC[sec])]))

            def v_sec(sec):
                W, wk = wget(("inv", sec), l)
                Wv = W[:, 0:4096].rearrange("p (k c) -> p k c", k=8)
                for tk4 in range(4):
                    si = sidx["i"] % 2
                    sidx["i"] += 1
                    for q in range(4):
                        tk = tk4 * 4 + q
                        b = bank_rr["i"] % 6
                        bank_rr["i"] += 1
                        for c in range(8):
                            P.op("pe", lambda e, Wv=Wv, c=c, b=b, tk=tk: e.matmul(
                                PS(b), lhsT=hT[:, c, tk * 128:(tk + 1) * 128], rhs=Wv[:, c, :], start=(c == 0), stop=(c == 7)),
                                rd=[wk, ("h", tk // 4)], wr=[pk(b)] if c == 0 else (), wrp=[pk(b)] if c else ())
                        P.op("act", lambda e, b=b, si=si, q=q: e.activation(out=stg[:, si, q * 512:(q + 1) * 512], in_=PS(b),
                                                                            func=AF.Identity),
                             rd=[pk(b)], wrp=[("stg", si)])
                    for j in range(4):
                        r0 = s1row(sec, j)
                        P.dma("sp", send1[r0:r0 + 128, tk4 * 512:(tk4 + 1) * 512].rearrange("p (k c) -> p k c", k=4),
                              stg[:, si, :].rearrange("p (k c) -> p k c", k=4)[:, :, j * 128:(j + 1) * 128],
                              ("stg", si), rd=[("stg", si)], wrp=[("s1", j // 2, sec)])
                for jp in range(2):
                    ag_defer.append(lambda jp=jp, sec=sec: allgather(send1, recv1, jp * 7 + SEC[sec], [("s1", jp, sec)],
                                                                      [("recv1", SEC[sec])]))

            def f_sec():
                W, wk = wget(("inf",), l)
                f_body(W, wk)

            def f_body(W, wk):
                Wv = W[:, 0:64].rearrange("p (k c) -> p k c", k=8)
                lfreg = send1l[0:16, :].bitcast(F32).rearrange("(h a) c -> h (a c)", h=8)
                for tt in range(4):
                    b = bank_rr["i"] % 6
                    bank_rr["i"] += 1
                    for c in range(8):
                        P.op("pe", lambda e, Wv=Wv, c=c, b=b, tt=tt: e.matmul(
                            ps[0:8, b * 512:(b + 1) * 512], lhsT=Wv[:, c, :], rhs=hT[:, c, tt * 512:(tt + 1) * 512],
                            start=(c == 0), stop=(c == 7)),
                            rd=[wk, ("h", tt)], wr=[pk(b)] if c == 0 else (), wrp=[pk(b)] if c else ())
                    u = tt % 2
                    P.op("act", lambda e, b=b, u=u: e.activation(out=f32a[0:8, u, :], in_=ps[0:8, b * 512:(b + 1) * 512], func=AF.Exp,
                                                                 bias=nfb[0:8, 0:1], scale=-1.0),
                         rd=[pk(b), "coef"], wr=[("f32a", u)])
                    P.op("act", lambda e, u=u: e.activation(out=f32a[0:8, u, :], in_=f32a[0:8, u, :], func=AF.Ln, bias=1.0, scale=1.0),
                         rd=[("f32a", u)], wr=[("f32a", u)])
                    P.op("dve", lambda e, u=u: e.tensor_scalar(out=tmpf[0:8, u, :], in0=f32a[0:8, u, :], scalar1=-1.0, scalar2=None,
                                                               op0=ALU.mult),
                         rd=[("f32a", u)], wr=[("tmpf", u)])
                    P.dma("sp", lfreg[:, tt * 512:(tt + 1) * 512], tmpf[0:8, u, :], ("tmpf", u), rd=[("tmpf", u)], wrp=["send1l"])
                ag_defer.append(lambda: P.dma(
                    "pool", None, None, "ccl", rd=["send1l"], wr=["recv1l"], inc=1,
                    fn=lambda e: e.collective_compute("AllGather", ALU.bypass, replica_groups=[[0, 1, 2, 3], [4, 5, 6, 7]],
                                                      ins=[send1l.ap().opt()], outs=[recv1l.ap().opt()])))

            ag_defer = []
            f_sec()
            feat_sec("QC")
            feat_sec("KC")
            v_sec("VC")
            c_ags = ag_defer
            ag_defer = []
            for f_ in c_ags[0:3]:
                f_()
            feat_sec("UB")
            for f_ in c_ags[3:5]:
                f_()
            feat_sec("QA")
            for f_ in c_ags[5:7]:
                f_()
            feat_sec("KA")
            v_sec("VA")
            for f_ in ag_defer:
                f_()
            if stop == "projA":
                return
            phaseB(l)
            if stop == "phaseB":
                return
            P.barrier()
            b2_ = recv2[bass.ds((jv // 2) * 3072 + (jv % 2) * 128, 128), :]
            P.dma("sp", loc2, bass.AP(tensor=b2_.tensor, offset=b2_.offset,
                                        ap=[[2048, 128], [256 * 2048, 12], [1, 2048]]),
                  "slab2", rd=["recv2"], wr=["loc2"])
            for hf in range(2):
                norm_mod(1, hTh, [2 * hf, 2 * hf + 1])
                P.dma("sp", ysb, loc2[:, :, hf * 1024:(hf + 1) * 1024],
                      "ldy", rd=["loc2"], wr=[("ysb", k_) for k_ in range(12)])
                for dc in range(8):
                    W, wk = wget(("mix", hf, dc), l)
                    Wv = W[:, 0:4608].rearrange("p (n k c) -> p n k c", n=3, k=12)
                    for n in range(3):
                        bs = 4 * (bank_rr["i"] % 2)
                        bank_rr["i"] += 1
                        for wc in range(4):
                            for t2 in range(2):
                                b = bs + t2
                                P.op("pe", lambda e, Wv=Wv, n=n, wc=wc, b=b, t2=t2: e.matmul(
                                    PS(b), lhsT=Wv[:, n, 8 + wc, :], rhs=ysb[:, n * 4 + wc, t2 * 512:(t2 + 1) * 512],
                                    start=(wc == 0), stop=(wc == 3)),
                                    rd=[wk, ("ysb", n * 4 + wc)], wr=[pk(b)] if wc == 0 else (), wrp=[pk(b)] if wc else ())
                        for kc in range(8):
                            for t2 in range(2):
                                b = bs + 2 + t2
                                P.op("pe", lambda e, Wv=Wv, n=n, kc=kc, b=b, t2=t2: e.matmul(
                                    PS(b), lhsT=Wv[:, n, kc, :], rhs=hTh[:, kc, t2 * 512:(t2 + 1) * 512],
                                    start=(kc == 0), stop=(kc == 7)),
                                    rd=[wk, ("h", t2)], wr=[pk(b)] if kc == 0 else (), wrp=[pk(b)] if kc else ())
                        for t2 in range(2):
                            P.op("act", lambda e, b=bs + 2 + t2, t2=t2: e.activation(out=f32a[:, t2, :], in_=PS(b), func=AF.Sigmoid),
                                 rd=[pk(bs + 2 + t2)], wr=[("f32a", t2)])
                            if n == 0:
                                P.op("dve", lambda e, b=bs + t2, t2=t2: e.tensor_tensor(out=rstd[:, t2, :], in0=PS(b), in1=f32a[:, t2, :],
                                                                                       op=ALU.mult),
                                     rd=[pk(bs + t2), ("f32a", t2)], wr=[("rstd", t2)])
                            else:
                                P.op("dve", lambda e, b=bs + t2, t2=t2: e.tensor_tensor(out=tmpf[:, t2, :], in0=PS(b), in1=f32a[:, t2, :],
                                                                                       op=ALU.mult),
                                     rd=[pk(bs + t2), ("f32a", t2)], wr=[("tmpf", t2)])
                                if n == 1:
                                    P.op("dve", lambda e, t2=t2: e.tensor_tensor(out=rstd[:, t2, :], in0=rstd[:, t2, :], in1=tmpf[:, t2, :],
                                                                                op=ALU.add),
                                         rd=[("rstd", t2), ("tmpf", t2)], wr=[("rstd", t2)])
                                else:
                                    P.op("dve", lambda e, t2=t2, dc=dc: e.tensor_tensor(out=mrg[:, dc, t2 * 512:(t2 + 1) * 512],
                                                                                       in0=rstd[:, t2, :], in1=tmpf[:, t2, :], op=ALU.add),
                                         rd=[("rstd", t2), ("tmpf", t2)], wr=[("mrg", dc, t2)])
                for oc2 in range(8):
                    if oc2 % 4 == 0:
                        W, wk = wget(("wo", hf, oc2 // 4), l)
                        Wv = W[:, 0:4096].rearrange("p (k c) -> p k c", k=8)
                    bs = 4 * (bank_rr["i"] % 2)
                    bank_rr["i"] += 1
                    for dc in range(8):
                        for t2 in range(2):
                            b = bs + t2
                            P.op("pe", lambda e, Wv=Wv, oc2=oc2, dc=dc, b=b, t2=t2: e.matmul(
                                PS(b), lhsT=Wv[:, dc, (oc2 % 4) * 128:(oc2 % 4 + 1) * 128], rhs=mrg[:, dc, t2 * 512:(t2 + 1) * 512],
                                start=(dc == 0), stop=(dc == 7)),
                                rd=[wk, ("mrg", dc, t2)], wr=[pk(b)] if dc == 0 else (), wrp=[pk(b)] if dc else ())
                    for t2 in range(2):
                        ts = slice(hf * 1024 + t2 * 512, hf * 1024 + (t2 + 1) * 512)
                        P.op("dve", lambda e, b=bs + t2, oc2=oc2, ts=ts: e.scalar_tensor_tensor(
                            out=xT[:, oc2, ts], in0=PS(b), scalar=ghalf[:, 8 + oc2:8 + oc2 + 1], in1=xT[:, oc2, ts],
                            op0=ALU.mult, op1=ALU.add),
                            rd=[pk(bs + t2), "coef"], wr=[("x", oc2)])
            if stop == "phaseC":
                return
            norm_mod(2, hT, range(4))
            ffn(1, 2)

        def attn_head(l, kind, hh, fillers, after_loads):
            qi, ki, vi = (0, 1, 2) if kind == 0 else (4, 5, 6)
            P.dma("sp", QT[0:64, :].rearrange("p (r t) -> p r t", r=4), loc1[hh * 64:(hh + 1) * 64, qi, :, :],
                  "ldq", rd=[("loc1", qi)], wrp=["QT"], wr=[("h", k_) for k_ in range(4)])
            P.dma("sp", KT[0:64, :].rearrange("p (r t) -> p r t", r=4), loc1[hh * 64:(hh + 1) * 64, ki, :, :],
                  "ldk", rd=[("loc1", ki)], wrp=["KT"], wr=[("h", k_) for k_ in range(4)])
            for r in range(4):
                P.dma("sp", VP[:, r * 16:(r + 1) * 16, 0:64],
                      loc1[:, vi, r, :].rearrange("p (k c) -> p k c", c=128)[:, :, hh * 64:(hh + 1) * 64],
                      "ldv", rd=[("loc1", vi)], wrp=["VP"], wr=[("stg", 0), ("stg", 1)])
            if kind == 1:
                P.op("dve", lambda e: e.memset(QT[64:70, :], 1.0), wrp=["QT"])
                P.op("dve", lambda e: e.memset(KT[64:70, :], 1.0), wrp=["KT"])
                P.dma("sp", QT[64:67, :], cqk[hh * 6:hh * 6 + 3, :], "ldq", rd=["cqk"], wr=["QT"])
                P.dma("sp", KT[67:70, :], cqk[hh * 6 + 3:hh * 6 + 6, :], "ldk", rd=["cqk"], wr=["KT"])
            if after_loads is not None:
                after_loads()
            nsec = 0 if kind == 0 else 2
            e0 = (l * 2 + hh) * 640

            if kind == 0:
                steps = [(G // 4, G) for G in range(64)]
                LA = 1
            else:
                steps = [(qt, kt) for qt in range(16) for kt in range(4 * qt + 4)]
                LA = 2
            n = len(steps)

            def front(si):
                qt, x = steps[si]
                pu = si % 4
                if kind == 0:
                    G = x
                    dds = [dd for dd in range(5) if G - dd >= 0]
                    nd = len(dds)
                    sbk = si % 2
                    pss = ps[:, sbk * 1024:sbk * 1024 + 640].rearrange("p (d c) -> p d c", d=5)
                    for dd in dds:
                        P.op("pe", lambda e, pss=pss, dd=dd, G=G: e.matmul(
                            pss[:, dd, :], lhsT=KT[0:64, (G - dd) * 128:(G - dd + 1) * 128], rhs=QT[0:64, G * 128:(G + 1) * 128],
                            start=True, stop=True),
                            rd=["QT", "KT"], wr=[pk(2 * sbk), pk(2 * sbk + 1)] if dd == 0 else (),
                            wrp=[pk(2 * sbk), pk(2 * sbk + 1)] if dd else ())
                    P.op("act", lambda e, pss=pss, nd=nd, pu=pu: e.activation(
                        out=Pe[:, pu % 2, 0:nd * 128].rearrange("p (d c) -> p d c", d=nd), in_=pss[:, 0:nd, :], func=AF.Exp),
                        rd=[pk(2 * sbk), pk(2 * sbk + 1)], wr=[("Pe", pu % 2)])
                    P.op("dve", lambda e, nd=nd, pu=pu: e.tensor_tensor(
                        out=Pt[:, pu, 0:nd * 128], in0=Pe[:, pu % 2, 0:nd * 128], in1=Eh[:, e0:e0 + nd * 128], op=ALU.mult),
                        rd=[("Pe", pu % 2), "Eh"], wr=[("Pt", pu)])
                else:
                    kt = x
                    sbk = si % 4
                    P.op("pe", lambda e, sbk=sbk, kt=kt, qt=qt: e.matmul(
                        PS(sbk), lhsT=KT[0:70, kt * 128:(kt + 1) * 128], rhs=QT[0:70, qt * 512:(qt + 1) * 512],
                        start=True, stop=True), rd=["QT", "KT"], wr=[pk(sbk)])
                    d = kt - 4 * qt
                    if d < 0:
                        P.op("act", lambda e, sbk=sbk, pu=pu: e.activation(out=Pt[:, pu, 0:512], in_=PS(sbk), func=AF.Exp),
                             rd=[pk(sbk)], wr=[("Pt", pu)])
                    else:
                        P.op("act", lambda e, sbk=sbk, pu=pu: e.activation(out=Pe[:, pu % 2, 0:512], in_=PS(sbk), func=AF.Exp),
                             rd=[pk(sbk)], wr=[("Pe", pu % 2)])
                        P.op("dve", lambda e, pu=pu, d=d: e.tensor_tensor(out=Pt[:, pu, 0:512], in0=Pe[:, pu % 2, 0:512],
                                                                         in1=masks[:, d * 512:(d + 1) * 512], op=ALU.min),
                             rd=[("Pe", pu % 2), "masks"], wr=[("Pt", pu)])

            def back(si):
                qt, x = steps[si]
                pu = si % 4
                bO = 6 + (qt % 2)
                if kind == 0:
                    G = x
                    gi = G % 4
                    dds = [dd for dd in range(5) if G - dd >= 0]
                    nd = len(dds)
                    for dd in dds:
                        P.op("pe", lambda e, dd=dd, G=G, gi=gi, pu=pu, nd=nd, bO=bO: e.matmul(
                            ps[0:65, bO * 512 + gi * 128: bO * 512 + (gi + 1) * 128], lhsT=VP[:, G - dd, :],
                            rhs=Pt[:, pu, dd * 128:(dd + 1) * 128], start=(dd == 0), stop=(dd == nd - 1)),
                            rd=["VP", ("Pt", pu)], wr=[pk(bO)] if (dd == 0 and gi == 0) else (),
                            wrp=[pk(bO)] if not (dd == 0 and gi == 0) else ())
                    return qt if gi == 3 else None
                kt = x
                nk = 4 * qt + 4
                P.op("pe", lambda e, kt=kt, pu=pu, nk=nk, bO=bO: e.matmul(
                    ps[0:65, bO * 512:(bO + 1) * 512], lhsT=VP[:, kt, :], rhs=Pt[:, pu, 0:512],
                    start=(kt == 0), stop=(kt == nk - 1)),
                    rd=["VP", ("Pt", pu)], wr=[pk(bO)] if kt == 0 else (), wrp=[pk(bO)] if kt else ())
                return qt if kt == nk - 1 else None

            def norm_a(qt):
                bO = 6 + (qt % 2)
                u = qt % 2
                P.op("dve", lambda e, u=u, bO=bO: e.reciprocal(out=f32a[64:65, u, :], in_=ps[64:65, bO * 512:(bO + 1) * 512]),
                     rd=[pk(bO)], wr=[("f32a", u)])

            def norm_b(qt):
                bO = 6 + (qt % 2)
                u = qt % 2
                bb = 4 + (qt % 2)
                P.op("pe", lambda e, u=u, bb=bb: e.matmul(ps[0:64, bb * 512:(bb + 1) * 512], lhsT=ones32[64:65, 0:64],
                                                         rhs=f32a[64:65, u, :], start=True, stop=True),
                     rd=["ones32", ("f32a", u)], wr=[pk(bb)])
                P.op("act", lambda e, u=u, bb=bb: e.activation(out=tmpf[0:64, u, :], in_=ps[0:64, bb * 512:(bb + 1) * 512],
                                                               func=AF.Identity),
                     rd=[pk(bb)], wr=[("tmpf", u)])
                P.op("dve", lambda e, u=u, bO=bO: e.tensor_tensor(out=yst[0:64, u, :], in0=ps[0:64, bO * 512:(bO + 1) * 512],
                                                                  in1=tmpf[0:64, u, :], op=ALU.mult),
                     rd=[pk(bO), ("tmpf", u)], wr=[("yst", u)])
                row = s2row(nsec, qt // 4) + hh * 64
                P.dma("sp", send2[row:row + 64, (qt % 4) * 512:(qt % 4 + 1) * 512], yst[0:64, u, :], ("yst", u),
                      rd=[("yst", u)], wrp=[("s2", (qt // 4) // 2, nsec)])

            pend = None
            for si in range(n + LA):
                if fillers is not None and si >= 2 and (kind == 0 or si % 8 == 0):
                    next(fillers, None)
                if si < n:
                    front(si)
                if pend is not None:
                    norm_b(pend)
                    pend = None
                if si - LA >= 0:
                    done = back(si - LA)
                    if done is not None:
                        norm_a(done)
                        pend = done
            if pend is not None:
                norm_b(pend)

        def fox_cumsum(l):
            w = bwork
            lf = w[:, 0:1024].bitcast(F32)
            cs = w[:, 1024:2048].bitcast(F32)
            spl = w[:, 2048:2048 + 3072].rearrange("p (s c) -> p s c", s=6)
            for hh in range(2):
                P.dma("sp", lf[hh * 16:(hh + 1) * 16, :], loclf[hh, :, :].rearrange("r (a t) -> (r a) t", a=4), "ldlf",
                      rd=["loclf"], wrp=["lf"])
            P.op("dve", lambda e: e.tensor_tensor_scan(out=cs[0:32, :], data0=lf[0:32, :], data1=lf[0:32, :], initial=0.0,
                                                       op0=ALU.add, op1=ALU.bypass), rd=["lf"], wr=["cs"])
            P.op("dve", lambda e: e.tensor_copy(out=carry[0:32, 0:1], in_=cs[0:32, 511:512]), rd=["cs"], wr=["carry"])
            P.op("pe", lambda e: e.matmul(ps[0:32, 0:1], lhsT=tri[0:32, 0:32], rhs=carry[0:32, 0:1], start=True, stop=True),
                 rd=["tri", "carry"], wr=[pk(0)])
            P.op("act", lambda e: e.activation(out=segoff[0:32, 0:1], in_=ps[0:32, 0:1], func=AF.Identity), rd=[pk(0)], wr=["offs"])
            P.op("dve", lambda e: e.tensor_scalar(out=cs[0:32, :], in0=cs[0:32, :], scalar1=segoff[0:32, 0:1], scalar2=None,
                                                  op0=ALU.add), rd=["cs", "offs"], wr=["cs"])
            P.op("dve", lambda e: e.tensor_copy(out=spl[0:32, 0, :], in_=cs[0:32, :]), rd=["cs"], wr=["spl0"])
            P.op("dve", lambda e: e.tensor_tensor(out=cs[0:32, :], in0=cs[0:32, :], in1=spl[0:32, 0, :], op=ALU.subtract),
                 rd=["cs", "spl0"], wr=["cs"])
            P.op("dve", lambda e: e.tensor_copy(out=spl[0:32, 1, :], in_=cs[0:32, :]), rd=["cs"], wr=["spl1"])
            P.op("dve", lambda e: e.tensor_tensor(out=cs[0:32, :], in0=cs[0:32, :], in1=spl[0:32, 1, :], op=ALU.subtract),
                 rd=["cs", "spl1"], wr=["cs"])
            P.op("dve", lambda e: e.tensor_copy(out=spl[0:32, 2, :], in_=cs[0:32, :]), rd=["cs"], wr=["spl2"])
            P.op("dve", lambda e: e.tensor_scalar(out=spl[0:32, 3:6, :], in0=spl[0:32, 0:3, :], scalar1=-1.0, scalar2=None,
                                                  op0=ALU.mult), rd=["spl0", "spl1", "spl2"], wr=["spl3"])
            for hh in range(2):
                P.dma("sp", cqk[hh * 6:(hh + 1) * 6, :].rearrange("s (g t) -> g s t", g=16), spl[hh * 16:(hh + 1) * 16, :, :],
                      "stspl", rd=["spl0", "spl1", "spl2", "spl3"], wrp=["cqk"])

        def pool_pieces(l):
            NP = 512
            w = bwork[:, 14 * 512:]
            u32 = w[:, 0:2 * (NP + 16)].bitcast(F32)
            o1 = 2 * (NP + 16)
            sA = w[:, o1:o1 + 2 * (NP + 16)].bitcast(F32)
            o1 += 2 * (NP + 16)
            sB = w[:, o1:o1 + 2 * (NP + 16)].bitcast(F32)
            o1 += 2 * (NP + 16)
            acc = w[:, o1:o1 + 2 * NP].bitcast(F32)
            o1 += 2 * NP
            ub = w[:, o1:o1 + NP]
            o1 += NP
            dd_ = w[:, o1:o1 + NP]
            o1 += NP
            assert o1 <= w.shape[1], (o1, w.shape)
            L = NP + 16
            def piece(pc):
                if pc == 0:
                    P.op("dve", lambda e: e.memset(u32[:, 0:16], 0.0), wr=["u32h"])
                r = pc // 4
                P.dma("sp", ub[:, :], loc1[:, 3, r, (pc % 4) * NP:(pc % 4 + 1) * NP], "ldub", rd=[("loc1", 3)], wr=["ub"])
                P.op("dve", lambda e: e.tensor_copy(out=u32[:, 16:L], in_=ub[:, :]), rd=["ub", "u32h"], wr=["u32"])
                P.op("dve", lambda e: e.tensor_tensor(out=sA[:, 1:L], in0=u32[:, 1:L], in1=u32[:, 0:L - 1], op=ALU.add),
                     rd=["u32"], wr=["sA"])
                P.op("dve", lambda e: e.tensor_scalar(out=acc[:, :], in0=sA[:, 16:L], scalar1=SM(l, 104, 105), scalar2=None,
                                                      op0=ALU.mult), rd=["sA", "small"], wr=["acc"])
                yield
                P.op("dve", lambda e: e.tensor_tensor(out=sB[:, 3:L], in0=sA[:, 3:L], in1=sA[:, 1:L - 2], op=ALU.add),
                     rd=["sA"], wr=["sB"])
                P.op("dve", lambda e: e.scalar_tensor_tensor(out=acc[:, :], in0=sB[:, 16:L], scalar=SM(l, 105, 106), in1=acc[:, :],
                                                             op0=ALU.mult, op1=ALU.add), rd=["sB", "small", "acc"], wr=["acc"])
                P.op("dve", lambda e: e.tensor_tensor(out=sA[:, 7:L], in0=sB[:, 7:L], in1=sB[:, 3:L - 4], op=ALU.add),
                     rd=["sB", "acc"], wr=["sA"])
                yield
                P.op("dve", lambda e: e.scalar_tensor_tensor(out=acc[:, :], in0=sA[:, 16:L], scalar=SM(l, 106, 107), in1=acc[:, :],
                                                             op0=ALU.mult, op1=ALU.add), rd=["sA", "small", "acc"], wr=["acc"])
                P.op("dve", lambda e: e.tensor_tensor(out=sB[:, 15:L], in0=sA[:, 15:L], in1=sA[:, 7:L - 8], op=ALU.add),
                     rd=["sA", "acc"], wr=["sB"])
                P.op("dve", lambda e: e.scalar_tensor_tensor(out=acc[:, :], in0=sB[:, 16:L], scalar=SM(l, 107, 108), in1=acc[:, :],
                                                             op0=ALU.mult, op1=ALU.add), rd=["sB", "small", "acc"], wr=["acc"])
                yield
                P.op("dve", lambda e: e.scalar_tensor_tensor(out=dd_[:, :], in0=acc[:, :], scalar=SM(l, 108, 109), in1=u32[:, 16:L],
                                                             op0=ALU.mult, op1=ALU.subtract), rd=["acc", "small", "u32"], wr=["dd"])
                if pc == 0:
                    P.op("dve", lambda e: e.tensor_tensor(out=acc[:, 0:16], in0=acc[:, 0:16], in1=SM(l, 112, 128), op=ALU.mult),
                         rd=["acc", "small", "dd"], wr=["acc"])
                    P.op("dve", lambda e: e.tensor_tensor(out=dd_[:, 0:16], in0=acc[:, 0:16], in1=u32[:, 16:32], op=ALU.subtract),
                         rd=["acc", "u32", "dd"], wr=["dd"])
                yield
                P.op("dve", lambda e: e.tensor_copy(out=u32[:, 0:16], in_=u32[:, NP:L]), rd=["u32", "dd"], wr=["u32h"])
                for t2 in range(NP // 512):
                    b = 4 + bank_rr["i"] % 2
                    bank_rr["i"] += 1
                    u = b % 2
                    P.op("pe", lambda e, b=b, t2=t2: e.matmul(PS(b), lhsT=wpool[:, l, :], rhs=dd_[:, t2 * 512:(t2 + 1) * 512],
                                                              start=True, stop=True), rd=["wpool", "dd"], wr=[pk(b)])
                    P.op("act", lambda e, b=b, u=u: e.activation(out=yst[:, u, :], in_=PS(b), func=AF.Identity,
                                                                 scale=SM(l, 101, 102)),
                         rd=[pk(b), "small"], wr=[("yst", u)])
                    tok = pc * NP + t2 * 512
                    row = s2row(1, tok // 2048)
                    P.dma("sp", send2[row:row + 128, tok % 2048:tok % 2048 + 512], yst[:, u, :], ("yst", u),
                          rd=[("yst", u)], wrp=[("s2", (tok // 2048) // 2, 1)])

                yield

            def gen():
                for pc in range(S // NP):
                    yield from piece(pc)
            return gen()

        def phaseB(l):
            def slab(s0, s1):
                b1 = recv1[bass.ds((jv // 2) * 7168 + (jv % 2) * 128 + s0 * 1024, 128), :]
                P.dma("sp", loc1[:, s0:s1, :, :].rearrange("p s r t -> p (s r) t"),
                      bass.AP(tensor=b1.tensor, offset=b1.offset, ap=[[2048, 128], [256 * 2048, 4 * (s1 - s0)], [1, 2048]]),
                      "slab1_%d" % s0, rd=[("recv1", s_) for s_ in range(s0, s1)], wr=[("loc1", s_) for s_ in range(s0, s1)])

            def later_slabs():
                bl = recv1l.ap().bitcast(F32).rearrange("(h a) c -> h (a c)", h=32)[bass.ds(jv * 2, 2), :]
                P.dma("sp", loclf.rearrange("h r t -> r h t"), bass.AP(tensor=bl.tensor, offset=bl.offset, ap=[[8 * 2048, 4], [2048, 2], [1, 2048]]),
                      "slabl", rd=["recv1l"], wr=["loclf"])
                slab(3, 4)

            bl = recv1l.ap().bitcast(F32).rearrange("(h a) c -> h (a c)", h=32)[bass.ds(jv * 2, 2), :]
            P.dma("sp", loclf.rearrange("h r t -> r h t"), bass.AP(tensor=bl.tensor, offset=bl.offset, ap=[[8 * 2048, 4], [2048, 2], [1, 2048]]),
                  "slabl", rd=["recv1l"], wr=["loclf"])
            fox_cumsum(l)
            slab(4, 7)
            P.op("dve", lambda e: e.memset(VP[:, :, 64:65], 1.0), wr=["VP", ("stg", 0), ("stg", 1)] + [("h", k_) for k_ in range(4)])
            fillers = pool_pieces(l)
            attn_head(l, 1, 0, fillers, lambda: slab(3, 4))
            attn_head(l, 1, 1, fillers, lambda: slab(0, 3))
            for qp in range(2):
                allgather(send2, recv2, qp * 3 + 2, [("s2", qp, 2)], ["recv2"])
            for _ in fillers:
                pass
            for qp in range(2):
                allgather(send2, recv2, qp * 3 + 1, [("s2", qp, 1)], ["recv2"])
            for hh in range(2):
                attn_head(l, 0, hh, None, None)
            if l + 1 < nlayers:
                mod_layer(l + 1)
            for qp in range(2):
                allgather(send2, recv2, qp * 3 + 0, [("s2", qp, 0)], ["recv2"])

        for l in range(nlayers):
            layer(l)
        for c in range(8):
            P.dma("sp", out_d[c * 128:(c + 1) * 128, :], xT[:, c, :], "out", rd=[("x", c)])
        if dbg:
            P.barrier()
            P.dma("sp", dbg1[:, :], send1[:, :], "out")
            P.dma("sp", dbg2[:, :], send2[:, :], "out")
        P.barrier()
        P.emit(stack)
    return nc


POOL_WINDOWS = (2, 4, 8, 16)


def prep_inputs(inp, nl=NL):
    W = {k: np.asarray(v) for k, v in inp.items()}
    items = weight_items(nl)
    wflat = np.concatenate([np.ascontiguousarray(host_item(n, l, W)).reshape(-1) for (n, l, _) in items]).astype(np.float32)
    s = np.arange(128)[:, None]
    t = np.arange(512)[None, :]
    m1 = np.stack([np.where(t >= 128 * d + s, 3.0e38, 0.0) for d in range(4)], axis=1).reshape(128, 2048)
    t1 = np.arange(128)[None, :]
    m2 = np.stack([((2 * dd + t1 // 64 - s // 64) >= 0) & ((2 * dd + t1 // 64 - s // 64) <= 8) for dd in range(5)],
                  axis=1).reshape(128, 640)
    masks = np.concatenate([m1, m2], axis=1).astype(np.float32).astype(ml_dtypes.bfloat16)
    kk = np.arange(32)
    tri = ((kk[:, None] // 16 == kk[None, :] // 16) & (kk[:, None] % 16 < kk[None, :] % 16)).astype(np.float32)
    maps = []
    for i in range(8):
        b, j = i // 4, i % 4
        xT = np.ascontiguousarray(W["x"][b, j * T:(j + 1) * T, :].T)
        cT = np.ascontiguousarray(W["c"][b].reshape(8, 128).T)
        small = np.zeros((NL, 128, NSM), np.float32)
        rb = np.zeros((128, NL, 2, 5, 128), np.float32)
        for l in range(NL):
            small[l, :, 0:72] = W["b_ada"][l].reshape(72, 128).T
            small[l, :, 72:96] = W["norm_gain"][l].reshape(24, 128).T
            small[l, :, 96:100] = np.tile(W["qk_gain"][l].T, (2, 1))
            small[l, 0:8, 100] = W["forget_bias"][l]
            small[l, :, 101] = W["pool_scale"][l, j * 128:(j + 1) * 128]
            small[l, :, 104 + j] = 1.0
            small[l, :, 108] = 1.0 / POOL_WINDOWS[j]
            small[l, :, 112:128] = 1.0 / np.minimum(np.arange(16) + 1, POOL_WINDOWS[j])[None, :]
            for hh in range(2):
                h = 2 * j + hh
                for dd in range(5):
                    rel = 128 * dd + t1 - s
                    idx = np.clip(rel, -128, 128) + 128
                    rb[:, l, hh, dd, :] = W["rel_bias"][l, h][idx]
        maps.append({
            "xT": xT, "cT": cT, "wflat": wflat, "small": small,
            "wpool": np.ascontiguousarray(W["w_pool"][:, j]),
            "rb": rb.reshape(128, -1), "masks": masks, "tri": tri,
        })
    return maps


_NC_CACHE = {}


def kernel(**inputs):
    maps = prep_inputs(inputs)
    if "nc" not in _NC_CACHE:
        _NC_CACHE["nc"] = build_nc()
    nc = _NC_CACHE["nc"]
    res = run_bass_kernel_spmd(nc, maps, core_ids=list(range(8)))
    out = np.zeros((2, S, D), np.float32)
    for i in range(8):
        b, j = i // 4, i % 4
        out[b, j * T:(j + 1) * T, :] = res.results[i]["outT"].T
    return out
```
